# Optimizing a Trainium2 kernel written in Bass

```python
import math
import jax, jax.numpy as jnp
from jax import lax
import numpy as np

D_MODEL = 2048
BATCH = 1
SEQ = 8192
DEPTH = 4

ROPE_THETA = 10000.0
NORM_EPS = 1e-6
RET_HEADS = 8
RET_DK = D_MODEL // 32
RET_DV = D_MODEL // 16
RET_CHUNK = 128
CONV_WIDTH = D_MODEL // 2
CONV_K = 3
DIFF_HEADS = 8
DIFF_HEAD_DIM = D_MODEL // (2 * DIFF_HEADS)
ATTN_BLOCK = 128
FFN_HIDDEN = -(-8 * D_MODEL // (3 * 256)) * 256
RET_QK = RET_HEADS * RET_DK
RET_V = RET_HEADS * RET_DV
HYB_IN = 2 * RET_QK + 2 * RET_V + 3 * CONV_WIDTH
HYB_CAT = RET_V + CONV_WIDTH
DIFF_QKV = 2 * DIFF_HEADS * 2 * DIFF_HEAD_DIM + DIFF_HEADS * 2 * DIFF_HEAD_DIM
DIFF_CAT = DIFF_HEADS * 2 * DIFF_HEAD_DIM
N_EVEN = (DEPTH + 1) // 2
N_ODD = DEPTH // 2

kernel_name = "hybrid_retention_shortconv_diffattn_swiglu"


def _rms(x):
    xf = x.astype(jnp.float32)
    return xf * lax.rsqrt(jnp.mean(xf * xf, axis=-1, keepdims=True) + NORM_EPS)


def rms_norm(x, g):
    return (_rms(x) * g.astype(jnp.float32)).astype(x.dtype)


def rope_tables(seq, dim):
    inv = ROPE_THETA ** (-jnp.arange(0, dim, 2, dtype=jnp.float32) / dim)
    ang = jnp.arange(seq, dtype=jnp.float32)[:, None] * inv[None, :]
    return jnp.cos(ang), jnp.sin(ang)


def apply_rope(x, cos, sin):
    half = x.shape[-1] // 2
    shape = (1, cos.shape[0]) + (1,) * (x.ndim - 3) + (half,)
    c = cos.reshape(shape).astype(x.dtype)
    s = sin.reshape(shape).astype(x.dtype)
    x1, x2 = x[..., :half], x[..., half:]
    return jnp.concatenate([x1 * c - x2 * s, x1 * s + x2 * c], axis=-1)


def retention_chunkwise(q, k, v):
    B, S, H, dk = q.shape
    dv = v.shape[-1]
    C = RET_CHUNK
    n = S // C
    gamma = 1.0 - jnp.exp2(-5.0 - jnp.arange(H, dtype=jnp.float32))
    log_g = jnp.log(gamma)
    idx = jnp.arange(C, dtype=jnp.float32)
    dist = idx[:, None] - idx[None, :]
    decay_in = jnp.where(dist >= 0, jnp.exp(log_g[:, None, None] * jnp.maximum(dist, 0.0)), 0.0)
    q_dec = jnp.exp(log_g[:, None] * (idx + 1.0))[:, :, None]
    k_dec = jnp.exp(log_g[:, None] * (C - 1.0 - idx))[:, :, None]
    chunk_dec = jnp.exp(log_g * C)[:, None, None]

    def to_chunks(t):
        return t.reshape(B, n, C, H, t.shape[-1]).transpose(1, 0, 3, 2, 4)

    def step(state, inp):
        qi, ki, vi = inp
        inner = jnp.einsum('bhid,bhjd->bhij', qi, ki) * decay_in
        o = (jnp.einsum('bhij,bhje->bhie', inner, vi)
             + jnp.einsum('bhid,bhde->bhie', qi * q_dec, state))
        state = chunk_dec * state + jnp.einsum('bhjd,bhje->bhde', ki * k_dec, vi)
        return state, o

    state0 = jnp.zeros((B, H, dk, dv), jnp.float32)
    _, o = lax.scan(step, state0, (to_chunks(q), to_chunks(k), to_chunks(v)))
    return o.transpose(1, 0, 3, 2, 4).reshape(B, S, H, dv)


def causal_depthwise_conv(u, w):
    return lax.conv_general_dilated(
        u, w[:, None, :].astype(u.dtype), window_strides=(1,), padding=[(CONV_K - 1, 0)],
        dimension_numbers=('NWC', 'WIO', 'NWC'), feature_group_count=u.shape[-1])


def hybrid_retention_conv(h, w_in, conv_w, w_out, cos, sin):
    B, S, _ = h.shape
    proj = h @ w_in
    cuts = np.cumsum([RET_QK, RET_QK, RET_V, RET_V, CONV_WIDTH, CONV_WIDTH]).tolist()
    q, k, v, g, cb, cc, cx = jnp.split(proj, cuts, axis=-1)
    q = apply_rope(q.reshape(B, S, RET_HEADS, RET_DK), cos, sin)
    k = apply_rope(k.reshape(B, S, RET_HEADS, RET_DK), cos, sin) * (RET_DK ** -0.5)
    ret = retention_chunkwise(q.astype(jnp.float32), k.astype(jnp.float32),
                              v.reshape(B, S, RET_HEADS, RET_DV).astype(jnp.float32))
    ret = _rms(ret).reshape(B, S, RET_V)
    ret = (jax.nn.silu(g.astype(jnp.float32)) * ret).astype(h.dtype)
    conv_out = cb * causal_depthwise_conv(cc * cx, conv_w)
    return jnp.concatenate([ret, conv_out], axis=-1) @ w_out


def diff_attention(h, w_qkv, q_norm, k_norm, lq1, lk1, lq2, lk2, subln, w_out,
                   lambda_init, cos, sin):
    B, S, _ = h.shape
    H, d = DIFF_HEADS, DIFF_HEAD_DIM
    q, k, v = jnp.split(h @ w_qkv, [2 * H * d, 4 * H * d], axis=-1)
    q = apply_rope(rms_norm(q.reshape(B, S, H, 2, d), q_norm), cos, sin)
    k = apply_rope(rms_norm(k.reshape(B, S, H, 2, d), k_norm), cos, sin)
    v = v.reshape(B, S, H, 2 * d)
    f32 = jnp.float32
    lam = (jnp.exp(jnp.sum(lq1.astype(f32) * lk1.astype(f32)))
           - jnp.exp(jnp.sum(lq2.astype(f32) * lk2.astype(f32))) + lambda_init)
    scale = d ** -0.5
    kpos = jnp.arange(S)

    def block(i):
        start = i * ATTN_BLOCK
        qb = lax.dynamic_slice_in_dim(q, start, ATTN_BLOCK, axis=1)
        s = jnp.einsum('bqhmd,bkhmd->bhmqk', qb, k, preferred_element_type=f32) * scale
        qpos = start + jnp.arange(ATTN_BLOCK)
        s = jnp.where(kpos[None, :] <= qpos[:, None], s, -jnp.inf)
        p = jax.nn.softmax(s, axis=-1)
        w = p[:, :, 0] - lam * p[:, :, 1]
        return jnp.einsum('bhqk,bkhe->bqhe', w.astype(v.dtype), v)

    o = lax.map(block, jnp.arange(S // ATTN_BLOCK))
    o = o.transpose(1, 0, 2, 3, 4).reshape(B, S, H, 2 * d)
    o = rms_norm(o, subln) * (1.0 - lambda_init)
    return o.reshape(B, S, DIFF_CAT) @ w_out


def swiglu(h, w_gate, w_up, w_down):
    return (jax.nn.silu(h @ w_gate) * (h @ w_up)) @ w_down


def setup_inputs(seed: int = 0) -> dict:
    key = jax.random.key(seed)
    ks = jax.random.split(key, 18)
    nrm = jax.random.normal
    f32 = jnp.float32

    def gain(k, shape):
        return 1.0 + 0.02 * nrm(k, shape, f32)

    return {
        "x": nrm(ks[0], (BATCH, SEQ, D_MODEL), f32),
        "norm_mix": gain(ks[1], (DEPTH, D_MODEL)),
        "norm_ffn": gain(ks[2], (DEPTH, D_MODEL)),
        "hyb_w_in": nrm(ks[3], (N_EVEN, D_MODEL, HYB_IN), f32) * D_MODEL ** -0.5,
        "hyb_conv_w": nrm(ks[4], (N_EVEN, CONV_K, CONV_WIDTH), f32) * CONV_K ** -0.5,
        "hyb_w_out": nrm(ks[5], (N_EVEN, HYB_CAT, D_MODEL), f32) * HYB_CAT ** -0.5,
        "diff_w_qkv": nrm(ks[6], (N_ODD, D_MODEL, DIFF_QKV), f32) * D_MODEL ** -0.5,
        "diff_q_norm": gain(ks[7], (N_ODD, DIFF_HEAD_DIM)),
        "diff_k_norm": gain(ks[8], (N_ODD, DIFF_HEAD_DIM)),
        "diff_lambda_q1": 0.1 * nrm(ks[9], (N_ODD, DIFF_HEAD_DIM), f32),
        "diff_lambda_k1": 0.1 * nrm(ks[10], (N_ODD, DIFF_HEAD_DIM), f32),
        "diff_lambda_q2": 0.1 * nrm(ks[11], (N_ODD, DIFF_HEAD_DIM), f32),
        "diff_lambda_k2": 0.1 * nrm(ks[12], (N_ODD, DIFF_HEAD_DIM), f32),
        "diff_subln": gain(ks[13], (N_ODD, 2 * DIFF_HEAD_DIM)),
        "diff_w_out": nrm(ks[14], (N_ODD, DIFF_CAT, D_MODEL), f32) * DIFF_CAT ** -0.5,
        "ffn_w_gate": nrm(ks[15], (DEPTH, D_MODEL, FFN_HIDDEN), f32) * D_MODEL ** -0.5,
        "ffn_w_up": nrm(ks[16], (DEPTH, D_MODEL, FFN_HIDDEN), f32) * D_MODEL ** -0.5,
        "ffn_w_down": nrm(ks[17], (DEPTH, FFN_HIDDEN, D_MODEL), f32) * FFN_HIDDEN ** -0.5,
    }


def reference(x, norm_mix, norm_ffn, hyb_w_in, hyb_conv_w, hyb_w_out,
              diff_w_qkv, diff_q_norm, diff_k_norm, diff_lambda_q1, diff_lambda_k1,
              diff_lambda_q2, diff_lambda_k2, diff_subln, diff_w_out,
              ffn_w_gate, ffn_w_up, ffn_w_down):
    S = x.shape[1]
    cos_r, sin_r = rope_tables(S, RET_DK)
    cos_a, sin_a = rope_tables(S, DIFF_HEAD_DIM)
    for layer in range(DEPTH):
        h = rms_norm(x, norm_mix[layer])
        j = layer // 2
        if layer % 2 == 0:
            x = x + hybrid_retention_conv(h, hyb_w_in[j], hyb_conv_w[j], hyb_w_out[j],
                                          cos_r, sin_r)
        else:
            lambda_init = 0.8 - 0.6 * math.exp(-0.3 * layer)
            x = x + diff_attention(h, diff_w_qkv[j], diff_q_norm[j], diff_k_norm[j],
                                   diff_lambda_q1[j], diff_lambda_k1[j],
                                   diff_lambda_q2[j], diff_lambda_k2[j],
                                   diff_subln[j], diff_w_out[j], lambda_init, cos_a, sin_a)
        x = x + swiglu(rms_norm(x, norm_ffn[layer]), ffn_w_gate[layer], ffn_w_up[layer],
                       ffn_w_down[layer])
    return x
```

```python
import math
from contextlib import ExitStack

import numpy as np
import ml_dtypes

import concourse.bass as bass
import concourse.mybir as mybir
from concourse.bass_utils import run_bass_kernel_spmd

F32 = mybir.dt.float32
BF16 = mybir.dt.bfloat16
AF = mybir.ActivationFunctionType
ALU = mybir.AluOpType
AX = mybir.AxisListType

NCORES = 8
D = 2048
S = 8192
NT = S // NCORES
KD = D // 128
FF = 5632
NFG = FF // 512
EPS = 1e-6
SEM_LIMIT = 24000
ARENA_BYTES = 204 * 1024


class Tile:
    __slots__ = ("w", "r", "pr", "name", "excl")

    def __init__(self, name="", excl=False):
        self.w = {}
        self.r = {}
        self.pr = {}
        self.name = name
        self.excl = excl


class Eng:
    def __init__(self, prog, name, be):
        self.prog = prog
        self.name = name
        self.be = be
        self.sem = prog.new_sem(name)
        self.count = 0
        self.known = {}
        self.dsems = []
        self.dcount = []
        self.di = 0


class Prog:
    def __init__(self, name):
        self.name = name
        self.nc = bass.Bass("TRN2", target_bir_lowering=False)
        self.es = ExitStack()
        self.nsem = 0
        self.eng = {}
        self.dry = False
        self.off = 0
        self.out_toks = []

    def new_sem(self, name):
        self.nsem += 1
        return self.es.enter_context(self.nc.semaphore(f"{name}_{self.nsem}"))

    def setup(self):
        nc = self.nc
        self.arena = self.es.enter_context(nc.sbuf_tensor("arena", [128, ARENA_BYTES // 4], F32))
        self.banks = [self.es.enter_context(nc.psum_tensor(f"bank{i}", [128, 512], F32)) for i in range(8)]
        self.bank_t = [Tile(f"bank{i}", excl=True) for i in range(8)]
        for nm, be in (("pe", nc.tensor), ("act", nc.scalar), ("dve", nc.vector),
                       ("pool", nc.gpsimd), ("sp", nc.sync)):
            self.eng[nm] = Eng(self, nm, be)
        for nm in ("pool", "sp"):
            e = self.eng[nm]
            for i in range(16):
                e.dsems.append(self.new_sem(f"d{nm}"))
                e.dcount.append(0)

    def din(self, name, shape, dt):
        return self.nc.dram_tensor(name, list(shape), dt, kind="ExternalInput").ap()

    def dout(self, name, shape, dt):
        return self.nc.dram_tensor(name, list(shape), dt, kind="ExternalOutput").ap()

    def alloc(self, nbytes):
        nbytes = (nbytes + 63) // 64 * 64
        off = self.off
        self.off += nbytes
        assert self.off <= ARENA_BYTES, (self.name, self.off)
        return off

    def view(self, off, dt, *shape):
        esz = 2 if dt == BF16 else 4
        n = int(np.prod(shape))
        a = self.arena[:, off // 4:(off + n * esz) // 4]
        if dt != F32:
            a = a.bitcast(dt)
        if len(shape) == 2:
            a = a.rearrange("p (a b) -> p a b", a=shape[0])
        elif len(shape) == 3:
            a = a.rearrange("p (a b c) -> p a b c", a=shape[0], b=shape[1])
        return a

    def buf(self, dt, *shape):
        esz = 2 if dt == BF16 else 4
        off = self.alloc(int(np.prod(shape)) * esz)
        return self.view(off, dt, *shape)

    def _need(self, E, reads, writes, acc=False):
        need = {}

        def add(tok):
            if tok is None:
                return
            s, v = tok
            if need.get(s, (None, 0))[1] < v:
                need[s] = (s, v)
        for t in reads:
            for tok in t.w.values():
                add(tok)
            if t.excl:
                for tok in t.r.values():
                    if tok[0] is not E.sem:
                        add(tok)
        for t in writes:
            if not acc:
                for tok in t.w.values():
                    add(tok)
            for tok in t.r.values():
                add(tok)
            if acc:
                for tok in t.pr.values():
                    add(tok)
        out = []
        for s, v in need.values():
            if E.name == "pe" and s is E.sem:
                continue
            k = id(s)
            if E.known.get(k, 0) < v:
                out.append((s, v))
                E.known[k] = v
        return out

    def _waits(self, E, reads, writes, acc=False):
        for s, v in self._need(E, reads, writes, acc):
            E.be.wait_ge(s, v)

    def _record(self, tok, reads, writes, acc=False):
        s, v = tok
        for t in reads:
            old = t.r.get(id(s))
            if old is None or old[1] < v:
                t.r[id(s)] = (s, v)
        for t in writes:
            if acc:
                old = t.w.get(id(s))
                if old is None or old[1] < v:
                    t.w[id(s)] = (s, v)
            else:
                t.w = {id(s): tok}
                t.pr = t.r
                t.r = {}

    def op(self, eng, emit, reads=(), writes=(), inc=True, acc=False):
        if self.dry:
            return
        E = self.eng[eng]
        need = self._need(E, reads, writes, acc)
        for sv in need[:-1]:
            E.be.wait_ge(*sv)
        ins = emit(E.be)
        if need:
            ins._wait_ge(*need[-1])
        if inc:
            E.count += 1
            ins.then_inc(E.sem, 1)
            tok = (E.sem, E.count)
        else:
            tok = (E.sem, E.count + 1)
        self._record(tok, reads, writes, acc)
        if inc and E.count >= SEM_LIMIT:
            E.sem = self.new_sem(E.name)
            E.count = 0

    def dma(self, q, out, in_, reads=(), writes=(), is_output=False, acc=False):
        if self.dry:
            return
        E = self.eng[q]
        self._waits(E, reads, writes, acc)
        i = E.di
        E.di = (E.di + 1) % len(E.dsems)
        if E.dcount[i] + 16 > SEM_LIMIT:
            E.dsems[i] = self.new_sem(f"d{q}")
            E.dcount[i] = 0
        if E.dcount[i] > 0 and E.known.get(id(E.dsems[i]), 0) < E.dcount[i]:
            E.be.wait_ge(E.dsems[i], E.dcount[i])
            E.known[id(E.dsems[i])] = E.dcount[i]
        ins = E.be.dma_start(out=out, in_=in_)
        E.dcount[i] += 16
        ins.then_inc(E.dsems[i], 16)
        tok = (E.dsems[i], E.dcount[i])
        self._record(tok, reads, writes, acc)
        if is_output:
            self.out_toks.append(tok)

    def barrier(self):
        if self.dry:
            return
        toks = []
        for E in self.eng.values():
            if E.count > 0:
                toks.append((E.sem, E.count))
            for s, c in zip(E.dsems, E.dcount):
                if c > 0:
                    toks.append((s, c))
        for E in self.eng.values():
            for s, v in toks:
                if E.name == "pe" and s is E.sem:
                    continue
                if E.known.get(id(s), 0) < v:
                    E.be.wait_ge(s, v)
                    E.known[id(s)] = v

    def finish(self):
        E = self.eng["sp"]
        for s, v in self.out_toks:
            if E.known.get(id(s), 0) < v:
                E.be.wait_ge(s, v)
                E.known[id(s)] = v

    def build(self, body):
        self.setup()
        block = self.es.enter_context(self.nc.Block())

        @block.sync
        def _(sp):
            body()
            self.finish()
        self.es.close()
        return self.nc


class WStream:
    def __init__(self, P, nslots):
        self.P = P
        self.n = nslots
        self.off = [P.alloc(16384) for _ in range(nslots)]
        self.tiles = [Tile(f"slot{i}") for i in range(nslots)]
        self.loads = []
        self.next_issue = 0
        self.next_acq = 0
        self.plan = None

    def _issue(self, idx):
        P = self.P
        spec = self.plan[idx]
        slot = idx % self.n
        kind, w, g = spec
        if kind == "cols":
            K = w.shape[0] // 128
            dst = P.view(self.off[slot], BF16, K, 512)
            src = w.rearrange("(k p) f -> p k f", p=128)
            for k0 in range(0, K, 4):
                P.dma("pool", dst[:, k0:k0 + 4, :], src[:, k0:k0 + 4, g * 512:(g + 1) * 512],
                      writes=[self.tiles[slot]], acc=(k0 > 0))
        else:
            dst = P.view(self.off[slot], BF16, 4, 2048)
            src = w.rearrange("(c p) f -> p c f", p=128)
            for c0 in range(4):
                P.dma("pool", dst[:, c0:c0 + 1, :], src[:, g * 4 + c0:g * 4 + c0 + 1, :],
                      writes=[self.tiles[slot]], acc=(c0 > 0))

    def acquire(self, kind, w, g):
        P = self.P
        if P.dry:
            self.loads.append((kind, w, g))
            if kind == "cols":
                return P.view(self.off[0], BF16, w.shape[0] // 128, 512), self.tiles[0]
            return P.view(self.off[0], BF16, 4, 2048), self.tiles[0]
        if self.plan is None:
            self.plan = self.loads
            while self.next_issue < min(self.n, len(self.plan)):
                self._issue(self.next_issue)
                self.next_issue += 1
        idx = self.next_acq
        self.next_acq += 1
        assert self.plan[idx][0] == kind and self.plan[idx][2] == g
        slot = idx % self.n
        if kind == "cols":
            K = w.shape[0] // 128
            v = P.view(self.off[slot], BF16, K, 512)
        else:
            v = P.view(self.off[slot], BF16, 4, 2048)
        return v, self.tiles[slot]

    def release(self):
        if self.P.dry:
            return
        if self.next_issue < len(self.plan):
            self._issue(self.next_issue)
            self.next_issue += 1


def run_two_pass(P, ws, body):
    P.dry = True
    body()
    P.dry = False
    body()


def emit_rstd(P, src, src_t, out, out_t, n):
    P.op("dve", lambda e: e.tensor_scalar(out, src, 1.0 / n, EPS, ALU.mult, ALU.add),
         reads=[src_t], writes=[out_t])
    P.op("dve", lambda e: e.reciprocal(out, out), reads=[out_t], writes=[out_t])
    P.op("act", lambda e: e.activation(out=out, in_=out, func=AF.Sqrt), reads=[out_t], writes=[out_t])


def emit_proj_fm(P, wv, wt, col0, H, Ht, bank, th, extra_reads=()):
    K = H.shape[1]
    for k in range(K):
        P.op("pe", lambda e, k=k: e.matmul(P.banks[bank][:], wv[:, k, col0:col0 + 128],
                                           H[:, k, th * 512:(th + 1) * 512],
                                           start=(k == 0), stop=(k == K - 1)),
             reads=[wt, Ht[k][th]] if k == 0 else [Ht[k][th]], writes=[P.bank_t[bank]],
             inc=(k == K - 1))


RA = 64 * 1024
RB = 128 * 1024
MISC = 192 * 1024
RET_H = 8
LAMBDA_INIT = {l: 0.8 - 0.6 * math.exp(-0.3 * l) for l in range(4)}


def T2(n, m=2):
    return [[Tile() for _ in range(m)] for _ in range(n)]


class Model:
    pass


def m_norm(M, X, Xt, g_d, H, Ht, soff):
    P = M.P
    M.sq = P.view(soff, BF16, 2, 512)
    M.rstd = P.view(soff + 2048, F32, 2, 512)
    gst = Tile()
    P.dma("sp", M.gs, g_d, writes=[gst])
    sqt = [Tile(), Tile()]
    rstdt = [Tile(), Tile()]
    for th in range(2):
        b = 6 + th
        for k in range(KD):
            i = k % 2
            P.op("act", lambda e: e.activation(out=M.sq[:, i, :], in_=X[:, k, th * 512:(th + 1) * 512],
                                               func=AF.Square), reads=[Xt[k][th]], writes=[sqt[i]])
            P.op("pe", lambda e: e.matmul(P.banks[b][:], M.ones, M.sq[:, i, :], start=(k == 0),
                                          stop=(k == KD - 1)),
                 reads=[sqt[i], M.const_t], writes=[P.bank_t[b]])
        emit_rstd(P, P.banks[b][:], P.bank_t[b], M.rstd[:, th, :], rstdt[th], D)
        for k in range(KD):
            P.op("dve", lambda e: e.scalar_tensor_tensor(
                out=H[:, k, th * 512:(th + 1) * 512], in0=X[:, k, th * 512:(th + 1) * 512],
                scalar=M.gs[:, k:k + 1], in1=M.rstd[:, th, :], op0=ALU.mult, op1=ALU.mult),
                reads=[Xt[k][th], rstdt[th], gst], writes=[Ht[k][th]])


def m_outffn(M, L, v, xin_d, xin_t, xout_d, xout_t, cat, catt, is_final):
    P, ws = M.P, M.ws
    X = P.view(RB, F32, KD, NT)
    Xt = T2(KD)
    for k in range(KD):
        P.dma("sp", X[:, k, :], xin_d[v, k], reads=[xin_t[v]], writes=[Xt[k][0], Xt[k][1]])
    u = 0
    for n in range(4):
        wv, wt = ws.acquire("cols", L["w_out"], n)
        for dc in range(4):
            for th in range(2):
                b = 4 + (u % 2)
                emit_proj_fm(P, wv, wt, dc * 128, cat, catt, b, th)
                k = n * 4 + dc
                P.op("dve", lambda e: e.tensor_tensor(
                    out=X[:, k, th * 512:(th + 1) * 512], in0=X[:, k, th * 512:(th + 1) * 512],
                    in1=P.banks[b][:], op=ALU.add), reads=[P.bank_t[b]], writes=[Xt[k][th]])
                u += 1
        ws.release()
    P.barrier()
    H = P.view(RA, BF16, KD, NT)
    Ht = T2(KD)
    m_norm(M, X, Xt, L["g_ffn"], H, Ht, RA + 56 * 1024)
    hm = P.view(RA + 32 * 1024, BF16, 2, 4, NT)
    hmt = [T2(4), T2(4)]
    sg = P.view(RA + 48 * 1024, F32, 2, 512)
    sgt = [Tile(), Tile()]
    ug = 0
    ud = 0
    for g in range(NFG):
        wgv, wgt = ws.acquire("cols", L["w_gate"], g)
        wuv, wut = ws.acquire("cols", L["w_up"], g)
        hb = g % 2
        for fc in range(4):
            for th in range(2):
                bg = ug % 2
                bu = 2 + ug % 2
                emit_proj_fm(P, wgv, wgt, fc * 128, H, Ht, bg, th)
                emit_proj_fm(P, wuv, wut, fc * 128, H, Ht, bu, th)
                si = ug % 2
                P.op("act", lambda e: e.activation(out=sg[:, si, :], in_=P.banks[bg][:], func=AF.Silu),
                     reads=[P.bank_t[bg]], writes=[sgt[si]])
                P.op("dve", lambda e: e.tensor_tensor(
                    out=hm[:, hb, fc, th * 512:(th + 1) * 512], in0=sg[:, si, :], in1=P.banks[bu][:],
                    op=ALU.mult), reads=[sgt[si], P.bank_t[bu]], writes=[hmt[hb][fc][th]])
                ug += 1
        ws.release()
        ws.release()
        wdv, wdt = ws.acquire("rows", L["w_down"], g)
        for dc in range(KD):
            for th in range(2):
                b = 4 + ud % 4
                for fc in range(4):
                    P.op("pe", lambda e: e.matmul(
                        P.banks[b][:], wdv[:, fc, dc * 128:(dc + 1) * 128],
                        hm[:, hb, fc, th * 512:(th + 1) * 512], start=(fc == 0), stop=(fc == 3)),
                        reads=[wdt, hmt[hb][fc][th]], writes=[P.bank_t[b]], inc=(fc == 3))
                P.op("dve", lambda e: e.tensor_tensor(
                    out=X[:, dc, th * 512:(th + 1) * 512], in0=X[:, dc, th * 512:(th + 1) * 512],
                    in1=P.banks[b][:], op=ALU.add), reads=[P.bank_t[b]], writes=[Xt[dc][th]])
                ud += 1
        ws.release()
    for k in range(KD):
        P.dma("sp", xout_d[v, k], X[:, k, :], reads=[Xt[k][0], Xt[k][1]], writes=[xout_t[v]],
              acc=(k > 0), is_output=is_final)
    P.barrier()


def m_rope(M, src_ps, src_t, dst, dst_t, C, S, tabt, perm, th, pre=None, post=None, post_t=None):
    P = M.P
    i = M.rc % 2
    M.rc += 1
    qf, qft = M.qf[:, i, :], M.qft[i]
    t1, t1t = M.t1[:, i, :], M.t1t[i]
    if pre is None:
        P.op("act", lambda e: e.activation(out=qf, in_=src_ps, func=AF.Copy), reads=[src_t], writes=[qft])
    else:
        P.op("act", lambda e: e.activation(out=qf, in_=src_ps, func=AF.Copy, scale=pre),
             reads=[src_t, M.lam_t], writes=[qft])
    b = 4 + i
    P.op("pe", lambda e: e.matmul(P.banks[b][:], perm, qf, start=True, stop=True),
         reads=[qft, M.const_t], writes=[P.bank_t[b]])
    sl = slice(th * 512, (th + 1) * 512)
    P.op("dve", lambda e: e.tensor_tensor(out=t1, in0=qf, in1=C[:, sl], op=ALU.mult),
         reads=[qft, tabt], writes=[t1t])
    P.op("dve", lambda e: e.tensor_tensor(out=qf, in0=P.banks[b][:], in1=S[:, sl], op=ALU.mult),
         reads=[P.bank_t[b], tabt], writes=[qft])
    if post is None:
        P.op("dve", lambda e: e.tensor_tensor(out=dst, in0=t1, in1=qf, op=ALU.add),
             reads=[t1t, qft], writes=[dst_t])
    else:
        P.op("dve", lambda e: e.tensor_tensor(out=t1, in0=t1, in1=qf, op=ALU.add),
             reads=[t1t, qft], writes=[t1t])
        P.op("dve", lambda e: e.tensor_tensor(out=dst, in0=t1, in1=post, op=ALU.mult),
             reads=[t1t, post_t], writes=[dst_t])


def m_hyb_block(M, L, v, xin_d, xin_t, xout_d, xout_t, is_final):
    P, ws = M.P, M.ws
    X = P.view(RA, F32, KD, NT)
    Xt = T2(KD)
    for k in range(KD):
        P.dma("sp", X[:, k, :], xin_d[v, k], reads=[xin_t[v]], writes=[Xt[k][0], Xt[k][1]])
    H = P.view(RB, BF16, KD, NT)
    Ht = T2(KD)
    m_norm(M, X, Xt, L["g_mix"], H, Ht, RB + 56 * 1024)
    xht, hht = Tile(), Tile()
    if v == 0:
        P.op("dve", lambda e: e.memset(M.hh, 0.0), writes=[hht])
    else:
        P.dma("sp", M.xh, xin_d[v - 1].rearrange("k p t -> p k t")[:, :, NT - 2:NT], reads=[xin_t[v - 1]],
              writes=[xht])
        P.op("dve", lambda e: e.tensor_tensor(out=M.xh2, in0=M.xh, in1=M.xh, op=ALU.mult),
             reads=[xht], writes=[hht])
        for k in range(KD):
            P.op("pe", lambda e: e.matmul(P.banks[7][:, 0:2], M.ones, M.xh2[:, k, :], start=(k == 0),
                                          stop=(k == KD - 1)), reads=[hht, M.const_t], writes=[P.bank_t[7]],
                 inc=(k == KD - 1))
        rt = Tile()
        emit_rstd(P, P.banks[7][:, 0:2], P.bank_t[7], M.rh, rt, D)
        gt = Tile()
        for k in range(KD):
            P.op("dve", lambda e: e.scalar_tensor_tensor(
                out=M.hh[:, k, :], in0=M.xh[:, k, :], scalar=M.gs[:, k:k + 1], in1=M.rh,
                op0=ALU.mult, op1=ALU.mult), reads=[xht, rt], writes=[hht])
    cwt = Tile()
    P.dma("sp", M.cw, L["conv_w"], writes=[cwt])
    P.barrier()
    if M.stop == "norm":
        return
    Qt_ = P.view(RA, BF16, 4, NT)
    Kt_ = P.view(RA + 8 * 1024, BF16, 4, NT)
    V = P.view(RA + 16 * 1024, BF16, 8, NT)
    cat = P.view(RA + 32 * 1024, BF16, KD, NT)
    Qt, Kt, Vt, catt = T2(4), T2(4), [Tile() for _ in range(8)], T2(KD)
    TR = RB + 32 * 1024
    ccu = P.view(TR, F32, 4, NT + 2)
    ccut = T2(4)
    ccht = [Tile() for _ in range(4)]
    yb = P.view(TR + 24 * 1024, F32, 2, 512)
    ybt = [Tile(), Tile()]
    tab = P.view(TR, F32, 2, 3, NT)
    M.qf = P.view(TR + 24 * 1024, F32, 2, 512)
    M.qft = [Tile(), Tile()]
    M.t1 = P.view(TR + 28 * 1024, F32, 2, 512)
    M.t1t = [Tile(), Tile()]
    w_in = L["w_in"]
    pb = 0
    for half in range(2):
        for kind, g in (("cc", 8 + half), ("cx", 10 + half), ("cb", 6 + half)):
            wv, wt = ws.acquire("cols", w_in, g)
            for mi in range(4):
                if kind != "cb":
                    b = pb % 4
                    pb += 1
                    for k in range(KD):
                        P.op("pe", lambda e: e.matmul(P.banks[b][:, 0:2], wv[:, k, mi * 128:(mi + 1) * 128],
                                                      M.hh[:, k, :], start=(k == 0), stop=(k == KD - 1)),
                             reads=[wt, hht], writes=[P.bank_t[b]], inc=(k == KD - 1))
                    if kind == "cc":
                        P.op("act", lambda e: e.activation(out=ccu[:, mi, 0:2], in_=P.banks[b][:, 0:2],
                                                           func=AF.Copy), reads=[P.bank_t[b]], writes=[ccht[mi]])
                    else:
                        P.op("dve", lambda e: e.tensor_tensor(out=ccu[:, mi, 0:2], in0=ccu[:, mi, 0:2],
                                                              in1=P.banks[b][:, 0:2], op=ALU.mult),
                             reads=[P.bank_t[b]], writes=[ccht[mi]])
                for th in range(2):
                    b = pb % 4
                    pb += 1
                    emit_proj_fm(P, wv, wt, mi * 128, H, Ht, b, th)
                    us = ccu[:, mi, 2 + th * 512:2 + (th + 1) * 512]
                    if kind == "cc":
                        P.op("act", lambda e: e.activation(out=us, in_=P.banks[b][:], func=AF.Copy),
                             reads=[P.bank_t[b]], writes=[ccut[mi][th]])
                    elif kind == "cx":
                        P.op("dve", lambda e: e.tensor_tensor(out=us, in0=us, in1=P.banks[b][:], op=ALU.mult),
                             reads=[P.bank_t[b]], writes=[ccut[mi][th]])
                    else:
                        m = half * 4 + mi
                        yi = (mi * 2 + th) % 2
                        y = yb[:, yi, :]
                        o = th * 512
                        rd = [ccut[mi][0], ccut[mi][1], ccht[mi], cwt]
                        P.op("dve", lambda e: e.tensor_scalar(y, ccu[:, mi, o + 2:o + 514], M.cw[:, m, 2:3], None,
                                                              ALU.mult), reads=rd, writes=[ybt[yi]])
                        P.op("dve", lambda e: e.scalar_tensor_tensor(
                            out=y, in0=ccu[:, mi, o + 1:o + 513], scalar=M.cw[:, m, 1:2], in1=y,
                            op0=ALU.mult, op1=ALU.add), reads=rd, writes=[ybt[yi]])
                        P.op("dve", lambda e: e.scalar_tensor_tensor(
                            out=y, in0=ccu[:, mi, o:o + 512], scalar=M.cw[:, m, 0:1], in1=y,
                            op0=ALU.mult, op1=ALU.add), reads=rd, writes=[ybt[yi]])
                        P.op("dve", lambda e: e.tensor_tensor(out=cat[:, 8 + m, o:o + 512], in0=y,
                                                              in1=P.banks[b][:], op=ALU.mult),
                             reads=[ybt[yi], P.bank_t[b]], writes=[catt[8 + m][th]])
            ws.release()
    P.barrier()
    if M.stop == "conv":
        return
    tabts = [Tile(), Tile()]
    for kind, g, dstv, dstt, decd in (("k", 1, Kt_, Kt, M.deck_d), ("q", 0, Qt_, Qt, M.decq_d)):
        wv, wt = ws.acquire("cols", w_in, g)
        for j in range(4):
            ti = M.tc % 2
            M.tc += 1
            tabt = tabts[ti]
            P.dma("sp", tab[:, ti, 0, :], M.rope_h_d[0][:, v * NT:(v + 1) * NT], writes=[tabt])
            P.dma("sp", tab[:, ti, 1, :], M.rope_h_d[1][:, v * NT:(v + 1) * NT], writes=[tabt], acc=True)
            P.dma("sp", tab[:, ti, 2, :], decd[j], writes=[tabt], acc=True)
            for th in range(2):
                b = pb % 4
                pb += 1
                emit_proj_fm(P, wv, wt, j * 128, H, Ht, b, th)
                m_rope(M, P.banks[b][:], P.bank_t[b], dstv[:, j, th * 512:(th + 1) * 512], dstt[j][th],
                       tab[:, ti, 0, :], tab[:, ti, 1, :], tabt, M.perm64, th,
                       post=tab[:, ti, 2, th * 512:(th + 1) * 512], post_t=tabt)
        ws.release()
    if M.stop == "qk":
        P.barrier()
        return
    for gi in range(2):
        wv, wt = ws.acquire("cols", w_in, 2 + gi)
        for tt in range(8):
            b = pb % 4
            pb += 1
            for k in range(KD):
                P.op("pe", lambda e: e.matmul(P.banks[b][:], H[:, k, tt * 128:(tt + 1) * 128], wv[:, k, :],
                                              start=(k == 0), stop=(k == KD - 1)),
                     reads=[wt, Ht[k][tt // 4]], writes=[P.bank_t[b]], inc=(k == KD - 1))
            P.op("act", lambda e: e.activation(out=V[:, tt, gi * 512:(gi + 1) * 512], in_=P.banks[b][:],
                                               func=AF.Copy), reads=[P.bank_t[b]], writes=[Vt[tt]])
        ws.release()
    for gi in range(2):
        wv, wt = ws.acquire("cols", w_in, 4 + gi)
        for hi in range(4):
            for th in range(2):
                b = pb % 4
                pb += 1
                emit_proj_fm(P, wv, wt, hi * 128, H, Ht, b, th)
                h = gi * 4 + hi
                P.op("act", lambda e: e.activation(out=cat[:, h, th * 512:(th + 1) * 512], in_=P.banks[b][:],
                                                   func=AF.Silu), reads=[P.bank_t[b]], writes=[catt[h][th]])
        ws.release()
    P.barrier()
    if M.stop == "inproj":
        return
    Ktok = P.view(RB, BF16, 8, 4, 128)
    Ktokt = [Tile() for _ in range(8)]
    PT = P.view(RB + 8 * 1024, BF16, 8, NT)
    osq = P.view(RB + 24 * 1024, BF16, 2, 512)
    osqt = [Tile(), Tile()]
    orstd = P.view(RB + 26 * 1024, F32, 2, 512)
    orstdt = [Tile(), Tile()]
    otmp = P.view(RB + 30 * 1024, F32, 2, 512)
    otmpt = [Tile(), Tile()]
    sib = P.view(RB + 34 * 1024, BF16, 4, 256)
    sibt = Tile()
    P.op("dve", lambda e: e.tensor_copy(sib, M.srun), reads=[M.srun_t], writes=[sibt])
    pbf = [P.banks[2][:].bitcast(BF16), P.banks[3][:].bitcast(BF16)]
    for tt in range(8):
        bi = tt % 2
        for j in range(4):
            P.op("pe", lambda e: e.transpose(pbf[bi][:, j * 128:(j + 1) * 128], Kt_[:, j, tt * 128:(tt + 1) * 128],
                                             M.ident), reads=[Kt[j][tt // 4], M.const_t], writes=[P.bank_t[2 + bi]],
                 inc=(j == 3))
        P.op("dve", lambda e: e.tensor_copy(Ktok[:, tt, :, :], pbf[bi][:, 0:512].rearrange("p (j d) -> p j d", j=4)),
             reads=[P.bank_t[2 + bi]], writes=[Ktokt[tt]])
    if M.stop == "ktok":
        P.barrier()
        return
    sc = 0
    oc = 0
    PTt = [Tile() for _ in range(8)]
    for h in range(RET_H):
        j, par = h // 2, h % 2
        ps = slice(64 * par, 64 * par + 64)
        for jb in range(8):
            i0 = jb * 128
            n = NT - i0
            for c0 in range(0, n, 512):
                cn = min(512, n - c0)
                b = sc % 2
                sc += 1
                P.op("pe", lambda e: e.matmul(P.banks[b][:, 0:cn], Kt_[ps, j, i0:i0 + 128],
                                              Qt_[ps, j, i0 + c0:i0 + c0 + cn], start=True, stop=True),
                     reads=[Kt[j][jb // 4], Qt[j][0], Qt[j][1]], writes=[P.bank_t[b]])
                first = (c0 == 0)
                if first:
                    P.op("dve", lambda e: e.tensor_tensor(out=PT[:, jb, i0:i0 + 128], in0=P.banks[b][:, 0:128],
                                                          in1=M.tri, op=ALU.mult),
                         reads=[P.bank_t[b], M.const_t], writes=[PTt[jb]])
                    if cn > 128:
                        P.op("act", lambda e: e.activation(out=PT[:, jb, i0 + 128:i0 + cn],
                                                           in_=P.banks[b][:, 128:cn], func=AF.Copy),
                             reads=[P.bank_t[b]], writes=[PTt[jb]], acc=True)
                else:
                    P.op("act", lambda e: e.activation(out=PT[:, jb, i0 + c0:i0 + c0 + cn],
                                                       in_=P.banks[b][:, 0:cn], func=AF.Copy),
                         reads=[P.bank_t[b]], writes=[PTt[jb]], acc=True)
        if M.stop == "r1":
            P.barrier()
            return
        for th in range(2):
            b = 4 + oc % 2
            i2 = oc % 2
            oc += 1
            for il in range(4):
                ib = th * 4 + il
                osl = P.banks[b][:, il * 128:(il + 1) * 128]
                for jb in range(ib + 1):
                    P.op("pe", lambda e: e.matmul(osl, V[:, jb, h * 128:(h + 1) * 128],
                                                  PT[:, jb, ib * 128:(ib + 1) * 128], start=(jb == 0),
                                                  stop=(M.stop == "r2" and jb == ib)),
                         reads=[Vt[jb], PTt[jb]], writes=[P.bank_t[b]], inc=(M.stop == "r2" and jb == ib and il == 3))
                if M.stop == "r2":
                    continue
                P.op("pe", lambda e: e.matmul(osl, sib[ps, j, par * 128:(par + 1) * 128],
                                              Qt_[ps, j, ib * 128:(ib + 1) * 128], start=False, stop=True),
                     reads=[sibt, Qt[j][th]], writes=[P.bank_t[b]], inc=(il == 3))
            if M.stop in ("r2", "r3"):
                continue
            P.op("act", lambda e: e.activation(out=osq[:, i2, :], in_=P.banks[b][:], func=AF.Square),
                 reads=[P.bank_t[b]], writes=[osqt[i2]])
            b2 = 6 + i2
            P.op("pe", lambda e: e.matmul(P.banks[b2][:], M.ones, osq[:, i2, :], start=True, stop=True),
                 reads=[osqt[i2], M.const_t], writes=[P.bank_t[b2]])
            emit_rstd(P, P.banks[b2][:], P.bank_t[b2], orstd[:, i2, :], orstdt[i2], 128)
            P.op("dve", lambda e: e.tensor_tensor(out=otmp[:, i2, :], in0=P.banks[b][:], in1=orstd[:, i2, :],
                                                  op=ALU.mult), reads=[P.bank_t[b], orstdt[i2]], writes=[otmpt[i2]])
            P.op("dve", lambda e: e.tensor_tensor(out=cat[:, h, th * 512:(th + 1) * 512], in0=otmp[:, i2, :],
                                                  in1=cat[:, h, th * 512:(th + 1) * 512], op=ALU.mult),
                 reads=[otmpt[i2]], writes=[catt[h][th]])
        if M.stop in ("r2", "r3"):
            P.barrier()
            return
    if v < M.NB - 1:
        for j in range(4):
            b = 2 + j % 2
            for tt in range(8):
                P.op("pe", lambda e: e.matmul(P.banks[b][:, 0:256], Ktok[:, tt, j, :], V[:, tt, j * 256:(j + 1) * 256],
                                              start=(tt == 0), stop=(tt == 7)),
                     reads=[Ktokt[tt], Vt[tt]], writes=[P.bank_t[b]], inc=(tt == 7))
            P.op("dve", lambda e: e.tensor_tensor(out=M.srun[:, j, :], in0=M.srun[:, j, :], in1=P.banks[b][:, 0:256],
                                                  op=ALU.add), reads=[P.bank_t[b], sibt], writes=[M.srun_t])
            P.op("dve", lambda e: e.tensor_scalar(M.srun[:, j, :], M.srun[:, j, :], M.dec1024[:, j:j + 1], None,
                                                  ALU.mult), reads=[M.const_t], writes=[M.srun_t])
    P.barrier()
    if M.stop == "ret":
        return
    m_outffn(M, L, v, xin_d, xin_t, xout_d, xout_t, cat, catt, is_final)


def m_lambda(M, L, l):
    P = M.P
    lt, st = Tile(), Tile()
    P.dma("sp", M.lamw, L["lamv"].rearrange("f p d -> p f d"), writes=[lt])
    for i in range(2):
        P.op("dve", lambda e: e.tensor_tensor(out=M.lamw[:, 2 * i, :], in0=M.lamw[:, 2 * i, :],
                                              in1=M.lamw[:, 2 * i + 1, :], op=ALU.mult), reads=[lt], writes=[lt])
        P.op("dve", lambda e: e.reduce_sum(out=M.ls[:, i:i + 1], in_=M.lamw[:, 2 * i, :], axis=AX.X),
             reads=[lt], writes=[st])
    P.op("act", lambda e: e.activation(out=M.ls[:, 0:2], in_=M.ls[:, 0:2], func=AF.Exp), reads=[st], writes=[st])
    P.op("dve", lambda e: e.scalar_tensor_tensor(out=M.ls[:, 2:3], in0=M.ls[:, 1:2], scalar=-LAMBDA_INIT[l],
                                                 in1=M.ls[:, 0:1], op0=ALU.add, op1=ALU.subtract),
         reads=[st], writes=[M.lam_t])
    P.dma("sp", M.gqk, L["qkn"], writes=[M.lam_t], acc=True)


def m_diff_qkv(M, L, v, xin_d, xin_t):
    P, ws = M.P, M.ws
    X = P.view(RA, F32, KD, NT)
    Xt = T2(KD)
    for k in range(KD):
        P.dma("sp", X[:, k, :], xin_d[v, k], reads=[xin_t[v]], writes=[Xt[k][0], Xt[k][1]])
    H = P.view(RB, BF16, KD, NT)
    Ht = T2(KD)
    m_norm(M, X, Xt, L["g_mix"], H, Ht, RB + 56 * 1024)
    P.barrier()
    TR = RB + 32 * 1024
    tab = P.view(TR, F32, 2, NT)
    tabt = Tile()
    P.dma("sp", tab[:, 0, :], M.rope_d_d[0][:, v * NT:(v + 1) * NT], writes=[tabt])
    P.dma("sp", tab[:, 1, :], M.rope_d_d[1][:, v * NT:(v + 1) * NT], writes=[tabt], acc=True)
    M.qf = P.view(TR + 8 * 1024, F32, 2, 512)
    M.qft = [Tile(), Tile()]
    M.t1 = P.view(TR + 12 * 1024, F32, 2, 512)
    M.t1t = [Tile(), Tile()]
    sqb = P.view(TR + 16 * 1024, BF16, 2, 512)
    sqbt = [Tile(), Tile()]
    rq = P.view(TR + 18 * 1024, F32, 2, 512)
    rqt = [Tile(), Tile()]
    qo = P.view(TR + 22 * 1024, BF16, 2, 512)
    qot = [Tile(), Tile()]
    vst = P.view(TR + 24 * 1024, BF16, 4, 512)
    vstt = [Tile() for _ in range(4)]
    w = L["w_qkv"]
    pb = 0
    cnt = 0
    first = True
    for kind in range(2):
        for gi in range(4):
            wv, wt = ws.acquire("cols", w, kind * 4 + gi)
            for ci in range(4):
                c = gi * 4 + ci
                for th in range(2):
                    b = pb % 4
                    pb += 1
                    i = cnt % 2
                    cnt += 1
                    emit_proj_fm(P, wv, wt, ci * 128, H, Ht, b, th)
                    P.op("act", lambda e: e.activation(out=sqb[:, i, :], in_=P.banks[b][:], func=AF.Square),
                         reads=[P.bank_t[b]], writes=[sqbt[i]])
                    P.op("pe", lambda e: e.matmul(P.banks[6 + i][:], M.ones, sqb[:, i, :], start=True, stop=True),
                         reads=[sqbt[i], M.const_t], writes=[P.bank_t[6 + i]])
                    emit_rstd(P, P.banks[6 + i][:], P.bank_t[6 + i], rq[:, i, :], rqt[i], 128)
                    m_rope(M, P.banks[b][:], P.bank_t[b], qo[:, i, :], qot[i], tab[:, 0, :], tab[:, 1, :], tabt,
                           M.perm128, th, pre=M.gqk[:, kind:kind + 1], post=rq[:, i, :], post_t=rqt[i])
                    P.dma("sp", M.qk_s[kind, c][:, v * NT + th * 512:v * NT + (th + 1) * 512], qo[:, i, :],
                          reads=[qot[i], M.lam_t], writes=[M.qk_t[v]], acc=not first)
                    first = False
            ws.release()
    first = True
    for gi in range(4):
        wv, wt = ws.acquire("cols", w, 8 + gi)
        for tt in range(8):
            b = pb % 4
            pb += 1
            i = (gi * 8 + tt) % 4
            for k in range(KD):
                P.op("pe", lambda e: e.matmul(P.banks[b][:], H[:, k, tt * 128:(tt + 1) * 128], wv[:, k, :],
                                              start=(k == 0), stop=(k == KD - 1)),
                     reads=[wt, Ht[k][tt // 4]], writes=[P.bank_t[b]], inc=(k == KD - 1))
            P.op("act", lambda e: e.activation(out=vst[:, i, :], in_=P.banks[b][:], func=AF.Copy),
                 reads=[P.bank_t[b]], writes=[vstt[i]])
            r0 = v * NT + tt * 128
            P.dma("sp", M.v_s[r0:r0 + 128, gi * 512:(gi + 1) * 512], vst[:, i, :], reads=[vstt[i]],
                  writes=[M.vs_t[v]], acc=not first)
            first = False
        ws.release()
    P.barrier()


def m_attention(M, L, l, h):
    P = M.P
    NB = M.NB
    NKB = NB * 8
    NTOK = NB * NT
    V = P.view(RA, BF16, NKB, 272)
    K = P.view(RB, BF16, 2, NTOK)
    Vt = [Tile() for _ in range(NB)]
    Kt = [Tile() for _ in range(NB)]
    o0 = RB + 2 * 2 * NTOK
    Qb = P.view(o0, BF16, 2, 2, NT)
    Qbt = [Tile(), Tile()]
    PTb = P.view(o0 + 8 * 1024, BF16, 2, 2, 256)
    PTbt = [Tile(), Tile()]
    rc = P.view(o0 + 10 * 1024, F32, 2, 8)
    rct = [Tile(), Tile()]
    a_ = P.view(o0 + 11 * 1024, F32, 2, 256)
    at = [Tile(), Tile()]
    sqv = P.view(o0 + 13 * 1024, F32, 2, 256)
    on = P.view(o0 + 15 * 1024, BF16, 2, 256)
    ont = [Tile(), Tile()]
    oT = P.view(o0 + 16 * 1024, BF16, 2, 2, 256)
    oTt = [Tile(), Tile()]
    gsub = P.view(o0 + 18 * 1024, F32, 256)
    gt = Tile()
    P.dma("sp", gsub, L["subln"], writes=[gt])
    P.op("dve", lambda e: e.tensor_scalar(gsub, gsub, float(1.0 - LAMBDA_INIT[l]), None, ALU.mult),
         reads=[gt], writes=[gt])
    ot = Tile()
    P.op("dve", lambda e: e.memset(V[:, :, 256:272], 1.0), writes=[ot])
    for v in range(NB):
        P.dma("sp", V[:, v * 8:(v + 1) * 8, 0:256],
              M.v_s[v * NT:(v + 1) * NT, h * 256:(h + 1) * 256].rearrange("(kb p) e -> p kb e", p=128),
              reads=[M.vs_t[v], ot], writes=[Vt[v]])
        for m in range(2):
            P.dma("sp", K[:, m, v * NT:(v + 1) * NT], M.qk_s[1, 2 * h + m][:, v * NT:(v + 1) * NT],
                  reads=[M.qk_t[v]], writes=[Kt[v]], acc=(m > 0))
    pbf = [P.banks[2][:].bitcast(BF16), P.banks[3][:].bitcast(BF16)]
    scale = float(128 ** -0.5)
    cnt = 0
    for G in range(NB * 4):
        v = G // 4
        qi = v % 2
        if G % 4 == 0:
            for m in range(2):
                P.dma("sp", Qb[:, qi, m, :], M.qk_s[0, 2 * h + m][:, v * NT:(v + 1) * NT],
                      reads=[M.qk_t[v]], writes=[Qbt[qi]], acc=(m > 0))
        q0 = (G % 4) * 256
        nkb = 2 * G + 2
        for kb in range(nkb):
            sb = cnt % 2
            pi = cnt % 2
            cnt += 1
            for m in range(2):
                P.op("pe", lambda e: e.matmul(P.banks[sb][:, m * 256:(m + 1) * 256], K[:, m, kb * 128:(kb + 1) * 128],
                                              Qb[:, qi, m, q0:q0 + 256], start=True, stop=True),
                     reads=[Kt[kb // 8], Qbt[qi]], writes=[P.bank_t[sb]], inc=(m == 1))
            P.op("act", lambda e: e.activation(out=PTb[:, pi, :, :].rearrange("p m q -> p (m q)"),
                                               in_=P.banks[sb][:], func=AF.Exp, scale=scale),
                 reads=[P.bank_t[sb]], writes=[PTbt[pi]])
            if kb >= 2 * G:
                b = kb - 2 * G
                for m in range(2):
                    P.op("dve", lambda e: e.tensor_tensor(
                        out=PTb[:, pi, m, b * 128:(b + 1) * 128], in0=PTb[:, pi, m, b * 128:(b + 1) * 128],
                        in1=M.tri_bf, op=ALU.mult), reads=[M.const_t], writes=[PTbt[pi]])
            for m in range(2):
                for b in range(2):
                    if kb == 2 * G + 1 and b == 0:
                        continue
                    ob = 4 + m * 2 + b
                    P.op("pe", lambda e: e.matmul(P.banks[ob][:, 0:257], PTb[:, pi, m, b * 128:(b + 1) * 128],
                                                  V[:, kb, 0:257], start=(kb == 0), stop=(kb == 2 * G + b)),
                         reads=[PTbt[pi], Vt[kb // 8], ot], writes=[P.bank_t[ob]])
        oi = G % 2
        for b in range(2):
            P.op("dve", lambda e: e.reciprocal(rc[:, b, 0:1], P.banks[4 + b][:, 256:257]),
                 reads=[P.bank_t[4 + b]], writes=[rct[b]])
            P.op("dve", lambda e: e.reciprocal(rc[:, b, 1:2], P.banks[6 + b][:, 256:257]),
                 reads=[P.bank_t[6 + b]], writes=[rct[b]])
            P.op("dve", lambda e: e.tensor_tensor(out=rc[:, b, 1:2], in0=rc[:, b, 1:2], in1=M.ls[:, 2:3],
                                                  op=ALU.mult), reads=[M.lam_t], writes=[rct[b]])
            P.op("dve", lambda e: e.tensor_scalar(a_[:, b, :], P.banks[4 + b][:, 0:256], rc[:, b, 0:1], None,
                                                  ALU.mult), reads=[P.bank_t[4 + b], rct[b]], writes=[at[b]])
            P.op("dve", lambda e: e.scalar_tensor_tensor(out=a_[:, b, :], in0=P.banks[6 + b][:, 0:256],
                                                         scalar=rc[:, b, 1:2], in1=a_[:, b, :],
                                                         op0=ALU.mult, op1=ALU.add),
                 reads=[P.bank_t[6 + b], rct[b]], writes=[at[b]])
            P.op("dve", lambda e: e.tensor_tensor(out=sqv[:, b, :], in0=a_[:, b, :], in1=a_[:, b, :], op=ALU.mult),
                 reads=[at[b]], writes=[ont[b]])
            P.op("dve", lambda e: e.reduce_sum(out=rc[:, b, 2:3], in_=sqv[:, b, :], axis=AX.X),
                 reads=[ont[b]], writes=[rct[b]])
            emit_rstd(P, rc[:, b, 2:3], rct[b], rc[:, b, 3:4], rct[b], 256)
            P.op("dve", lambda e: e.scalar_tensor_tensor(out=on[:, b, :], in0=a_[:, b, :], scalar=rc[:, b, 3:4],
                                                         in1=gsub, op0=ALU.mult, op1=ALU.mult),
                 reads=[at[b], rct[b], gt], writes=[ont[b]])
            for half in range(2):
                P.op("pe", lambda e: e.transpose(pbf[b][:, half * 128:(half + 1) * 128],
                                                 on[:, b, half * 128:(half + 1) * 128], M.ident),
                     reads=[ont[b], M.const_t], writes=[P.bank_t[2 + b]], inc=(half == 1))
            P.op("act", lambda e: e.activation(out=oT[:, oi, :, b * 128:(b + 1) * 128],
                                               in_=pbf[b][:, 0:256].rearrange("p (h q) -> p h q", h=2), func=AF.Copy),
                 reads=[P.bank_t[2 + b]], writes=[oTt[oi]], acc=(b == 1))
        for half in range(2):
            P.dma("sp", M.o_s[2 * h + half][:, G * 256:(G + 1) * 256], oT[:, oi, half, :], reads=[oTt[oi]],
                  writes=[M.os_t[v]], acc=True)
    P.barrier()


def m_diff_out(M, L, v, xin_d, xin_t, xout_d, xout_t, is_final):
    P = M.P
    cat = P.view(RA + 32 * 1024, BF16, KD, NT)
    catt = T2(KD)
    for ch in range(KD):
        P.dma("sp", cat[:, ch, :], M.o_s[ch][:, v * NT:(v + 1) * NT], reads=[M.os_t[v]],
              writes=[catt[ch][0], catt[ch][1]])
    m_outffn(M, L, v, xin_d, xin_t, xout_d, xout_t, cat, catt, is_final)


def build_model(layers=(0, 1, 2, 3), NB=8, stop=None):
    P = Prog("model")
    M = Model()
    M.P, M.NB = P, NB
    M.stop = stop
    NTOK = NB * NT
    x_d = P.din("x", [NB, KD, 128, NT], F32)
    y_d = P.dout("y", [NB, KD, 128, NT], F32)
    cbf_d = P.din("cbf", [128, 3, 128], BF16)
    cf_d = P.din("cf", [128, 3, 128], F32)
    M.rope_h_d = P.din("rope_h", [2, 128, NTOK], F32)
    M.rope_d_d = P.din("rope_d", [2, 128, NTOK], F32)
    M.decq_d = P.din("decq", [4, 128, NT], F32)
    M.deck_d = P.din("deck", [4, 128, NT], F32)
    dec1024_d = P.din("dec1024", [128, 4], F32)
    LD = {}
    for l in layers:
        d = {"g_mix": P.din(f"g_mix{l}", [128, KD], F32), "g_ffn": P.din(f"g_ffn{l}", [128, KD], F32),
             "w_out": P.din(f"w_out{l}", [D, D], F32), "w_gate": P.din(f"w_gate{l}", [D, FF], F32),
             "w_up": P.din(f"w_up{l}", [D, FF], F32), "w_down": P.din(f"w_down{l}", [FF, D], F32)}
        if l % 2 == 0:
            d["w_in"] = P.din(f"w_in{l}", [D, 6144], F32)
            d["conv_w"] = P.din(f"conv_w{l}", [128, 8, 3], F32)
        else:
            d["w_qkv"] = P.din(f"w_qkv{l}", [D, 6144], F32)
            d["lamv"] = P.din(f"lamv{l}", [4, 128, 128], F32)
            d["qkn"] = P.din(f"qkn{l}", [128, 2], F32)
            d["subln"] = P.din(f"subln{l}", [128, 256], F32)
        LD[l] = d
    nc = P.nc
    Xs = [nc.dram_tensor(f"xs{i}", [NB, KD, 128, NT], F32).ap() for i in range(2)]
    if any(l % 2 == 1 for l in layers):
        M.qk_s = nc.dram_tensor("qk_s", [2, 16, 128, NTOK], BF16).ap()
        M.v_s = nc.dram_tensor("v_s", [NTOK, 2048], BF16).ap()
        M.o_s = nc.dram_tensor("o_s", [16, 128, NTOK], BF16).ap()
    M.qk_t = [Tile() for _ in range(NB)]
    M.vs_t = [Tile() for _ in range(NB)]
    M.os_t = [Tile() for _ in range(NB)]

    def outer():
        ws = WStream(P, 4)
        M.ws = ws
        o = MISC
        cb = P.view(o, BF16, 3, 128)
        M.ones, M.ident, M.tri_bf = cb[:, 0, :], cb[:, 1, :], cb[:, 2, :]
        o += 768
        cf = P.view(o, F32, 3, 128)
        M.perm64, M.perm128, M.tri = cf[:, 0, :], cf[:, 1, :], cf[:, 2, :]
        o += 1536
        M.gs = P.view(o, F32, KD); o += 64
        M.cw = P.view(o, F32, 8, 3); o += 128
        M.srun = P.view(o, F32, 4, 256); o += 4096
        M.dec1024 = P.view(o, F32, 4); o += 64
        M.xh = P.view(o, F32, KD, 2); o += 128
        M.xh2 = P.view(o, BF16, KD, 2); o += 64
        M.hh = P.view(o, BF16, KD, 2); o += 64
        M.rh = P.view(o, F32, 2); o += 64
        M.ls = P.view(o, F32, 8); o += 64
        M.gqk = P.view(o, F32, 2); o += 64
        M.lamw = P.view(o, F32, 4, 128); o += 2048
        assert o <= ARENA_BYTES
        M.const_t = Tile()
        M.srun_t = Tile()
        M.lam_t = Tile()

        def body():
            M.rc = 0
            M.tc = 0
            P.dma("sp", cb, cbf_d, writes=[M.const_t])
            P.dma("sp", cf, cf_d, writes=[M.const_t], acc=True)
            P.dma("sp", M.dec1024, dec1024_d, writes=[M.const_t], acc=True)
            xin_d, xin_t = x_d, [Tile() for _ in range(NB)]
            for li, l in enumerate(layers):
                last = li == len(layers) - 1
                xout_d = y_d if last else Xs[li % 2]
                xout_t = [Tile() for _ in range(NB)]
                L = LD[l]
                if l % 2 == 0:
                    P.op("dve", lambda e: e.memset(M.srun, 0.0), writes=[M.srun_t])
                    for v in range(NB):
                        m_hyb_block(M, L, v, xin_d, xin_t, xout_d, xout_t, last)
                else:
                    m_lambda(M, L, l)
                    for v in range(NB):
                        m_diff_qkv(M, L, v, xin_d, xin_t)
                    for h in range(8):
                        m_attention(M, L, l, h)
                    for v in range(NB):
                        m_diff_out(M, L, v, xin_d, xin_t, xout_d, xout_t, last)
                xin_d, xin_t = xout_d, xout_t
        run_two_pass(P, ws, body)
    return P.build(outer)


def _rope_tables(dim, npos):
    inv = (10000.0 ** (-np.arange(0, dim, 2, dtype=np.float32) / np.float32(dim))).astype(np.float32)
    ang = (np.arange(npos, dtype=np.float32)[:, None] * inv[None, :]).astype(np.float32)
    c = np.cos(ang.astype(np.float64)).T
    s = np.sin(ang.astype(np.float64)).T
    half = dim // 2
    reps = 128 // dim
    C = np.concatenate([c, c] * reps, 0)
    Sg = np.concatenate([-s, s] * reps, 0)
    return np.stack([C, Sg]).astype(np.float32)


def _consts(NB):
    ntok = NB * NT
    ident = np.eye(128, dtype=np.float32)
    tri = (np.arange(128)[None, :] >= np.arange(128)[:, None]).astype(np.float32)
    cbf = np.stack([np.ones((128, 128), np.float32), ident, tri], 1).astype(ml_dtypes.bfloat16)
    perm64 = np.zeros((128, 128), np.float32)
    perm128 = np.zeros((128, 128), np.float32)
    for p in range(128):
        a, i = divmod(p, 64)
        perm64[64 * a + (i + 32) % 64, p] = 1.0
        perm128[(p + 64) % 128, p] = 1.0
    cf = np.stack([perm64, perm128, tri], 1).astype(np.float32)
    gam = 1.0 - np.exp2(-5.0 - np.arange(8, dtype=np.float64))
    t = np.arange(NT, dtype=np.float64)
    decq = np.zeros((4, 128, NT), np.float64)
    deck = np.zeros((4, 128, NT), np.float64)
    dec1024 = np.zeros((128, 4), np.float64)
    for j in range(4):
        for a in range(2):
            g = gam[2 * j + a]
            decq[j, 64 * a:64 * a + 64, :] = g ** t
            deck[j, 64 * a:64 * a + 64, :] = g ** (-t) * (64 ** -0.5)
            dec1024[64 * a:64 * a + 64, j] = g ** NT
    return {"cbf": cbf, "cf": cf, "rope_h": _rope_tables(64, ntok), "rope_d": _rope_tables(128, ntok),
            "decq": decq.astype(np.float32), "deck": deck.astype(np.float32),
            "dec1024": dec1024.astype(np.float32)}


def _fm(a):
    nb = a.shape[0] // NT
    return np.ascontiguousarray(a.reshape(nb, NT, KD, 128).transpose(0, 2, 3, 1))


def _unfm(y):
    nb = y.shape[0]
    return np.ascontiguousarray(y.transpose(0, 3, 1, 2).reshape(nb * NT, D))


def _gvec(g):
    return np.ascontiguousarray(np.asarray(g, np.float32).reshape(KD, 128).T)


def make_inputs(inp, layers, NB):
    m = dict(_consts(NB))
    m["x"] = _fm(np.asarray(inp["x"], np.float32)[0, :NB * NT])
    for l in layers:
        j = l // 2
        m[f"g_mix{l}"] = _gvec(inp["norm_mix"][l])
        m[f"g_ffn{l}"] = _gvec(inp["norm_ffn"][l])
        m[f"w_gate{l}"] = np.asarray(inp["ffn_w_gate"][l], np.float32)
        m[f"w_up{l}"] = np.asarray(inp["ffn_w_up"][l], np.float32)
        m[f"w_down{l}"] = np.asarray(inp["ffn_w_down"][l], np.float32)
        if l % 2 == 0:
            m[f"w_in{l}"] = np.asarray(inp["hyb_w_in"][j], np.float32)
            m[f"w_out{l}"] = np.asarray(inp["hyb_w_out"][j], np.float32)
            cw = np.asarray(inp["hyb_conv_w"][j], np.float32)
            m[f"conv_w{l}"] = np.ascontiguousarray(cw.reshape(3, 8, 128).transpose(2, 1, 0))
        else:
            m[f"w_qkv{l}"] = np.asarray(inp["diff_w_qkv"][j], np.float32)
            m[f"w_out{l}"] = np.asarray(inp["diff_w_out"][j], np.float32)
            lv = np.stack([inp["diff_lambda_q1"][j], inp["diff_lambda_k1"][j],
                           inp["diff_lambda_q2"][j], inp["diff_lambda_k2"][j]]).astype(np.float32)
            m[f"lamv{l}"] = np.ascontiguousarray(np.broadcast_to(lv[:, None, :], (4, 128, 128)))
            m[f"qkn{l}"] = np.ascontiguousarray(np.stack([inp["diff_q_norm"][j], inp["diff_k_norm"][j]], 1)
                                                .astype(np.float32))
            m[f"subln{l}"] = np.ascontiguousarray(np.broadcast_to(
                np.asarray(inp["diff_subln"][j], np.float32)[None, :], (128, 256)))
    return m


_NC_CACHE = {}


def run_model(inp, layers=(0, 1, 2, 3), NB=8, trace=False, stop=None):
    key = (tuple(layers), NB, stop)
    if key not in _NC_CACHE:
        _NC_CACHE[key] = build_model(layers, NB, stop)
    nc = _NC_CACHE[key]
    res = run_bass_kernel_spmd(nc, [make_inputs(inp, layers, NB)], core_ids=[0], trace=trace)
    return _unfm(res.results[0]["y"]), res


def kernel(**inputs):
    y, _ = run_model(inputs)
    return y[None].astype(np.float32)
```
